# Optimizing a Trainium2 kernel written in Bass

```python
import math
import jax, jax.numpy as jnp
from jax import lax
import numpy as np

D_MODEL = 2048
BATCH = 2
SEQ = 8192
DEPTH = 1
DEC_BATCH = 2
DEC_SEQ = 4096
PAST_LEN = 128

N_FNET_GROUPS = 4
FNET_GROUP_DIM = 256
D_FNET = N_FNET_GROUPS * FNET_GROUP_DIM
D_HYENA = D_MODEL // 2
HYENA_ORDER = 2
D_IN = D_FNET + (HYENA_ORDER + 1) * D_HYENA
SHORT_CONV = 3
FILTER_EMB_DIM = 33
FILTER_BANDS = (FILTER_EMB_DIM - 1) // 2
FILTER_HIDDEN = 64
DECAY_TARGET = 1e-2
FAST_DECAY_PCT = 0.3
SLOW_DECAY_PCT = 1.5
D_FF = 4 * D_MODEL
N_BRANCHES = 2
EPS = 1e-6

kernel_name = "hybrid_fnet_hyena_encoder"


def _rmsnorm(x, g):
    x32 = x.astype(jnp.float32)
    y = x32 * lax.rsqrt(jnp.mean(x32 * x32, axis=-1, keepdims=True) + EPS)
    return y.astype(x.dtype) * g


def _fnet_branch(a, w_map):
    b, l, _ = a.shape
    a4 = a.reshape(b, l, N_FNET_GROUPS, FNET_GROUP_DIM).astype(jnp.float32)
    mixed = jnp.fft.fft2(a4, axes=(1, 3), norm="ortho").real
    mixed = mixed.reshape(b, l, D_FNET).astype(a.dtype)
    return mixed @ w_map


def _hyena_filter(l, w1, b1, w2, b2, w3, b3, w4, freq):
    f32 = jnp.float32
    t = jnp.linspace(0.0, 1.0, l, dtype=f32)[:, None]
    pos = jnp.arange(l, dtype=f32)[:, None]
    bands = jnp.linspace(1e-4, FILTER_BANDS - 1, FILTER_BANDS, dtype=f32)[None, :]
    ang = (2.0 * math.pi / l) * pos * bands
    z = jnp.concatenate([t, jnp.cos(ang), -jnp.sin(ang)], axis=-1)
    fr = freq.astype(f32)
    h = jnp.sin(fr * (z @ w1.astype(f32) + b1.astype(f32)))
    h = jnp.sin(fr * (h @ w2.astype(f32) + b2.astype(f32)))
    h = jnp.sin(fr * (h @ w3.astype(f32) + b3.astype(f32)))
    h = h @ w4.astype(f32)
    deltas = jnp.abs(jnp.linspace(math.log(DECAY_TARGET) / FAST_DECAY_PCT,
                                  math.log(DECAY_TARGET) / SLOW_DECAY_PCT, D_HYENA, dtype=f32))
    decay = jnp.exp(-t * deltas[None, :])
    h_fwd = h[:, :D_HYENA] * decay
    h_bwd = h[:, D_HYENA:] * decay
    k = jnp.concatenate([h_fwd, jnp.zeros((1, D_HYENA), f32), h_bwd[:0:-1]], axis=0)
    return k * lax.rsqrt(jnp.sum(k * k, axis=0, keepdims=True) + EPS)


def _hyena_branch(u, conv_w, conv_b, w1, b1, w2, b2, w3, b3, w4, freq, skip, w_out):
    l = u.shape[1]
    up = jnp.pad(u, ((0, 0), (1, 1), (0, 0)))
    u = up[:, :-2] * conv_w[0] + up[:, 1:-1] * conv_w[1] + up[:, 2:] * conv_w[2] + conv_b
    x1, x2, v = jnp.split(u, 3, axis=-1)
    v32 = (v * x2).astype(jnp.float32)
    k = _hyena_filter(l, w1, b1, w2, b2, w3, b3, w4, freq)
    v_f = jnp.fft.rfft(v32, n=2 * l, axis=1)
    k_f = jnp.fft.rfft(k, n=2 * l, axis=0)
    y = jnp.fft.irfft(v_f * k_f[None], n=2 * l, axis=1)[:, :l]
    y = y + v32 * skip.astype(jnp.float32)
    y = y.astype(u.dtype) * x1
    return y @ w_out


def _layer(x, norm1_g, w_in, conv_w, conv_b, filt_w1, filt_b1, filt_w2, filt_b2, filt_w3, filt_b3,
           filt_w4, filt_freq, hyena_skip, w_fnet_map, w_hyena_out, w_gate, b_gate, w_out,
           norm2_g, w_mlp1, w_mlp2):
    h = _rmsnorm(x, norm1_g)
    proj = h @ w_in
    y_a = _fnet_branch(proj[..., :D_FNET], w_fnet_map)
    y_b = _hyena_branch(proj[..., D_FNET:], conv_w, conv_b, filt_w1, filt_b1, filt_w2, filt_b2,
                        filt_w3, filt_b3, filt_w4, filt_freq, hyena_skip, w_hyena_out)
    gates = jax.nn.sigmoid(h @ w_gate + b_gate)
    g_a, g_b = jnp.split(gates, N_BRANCHES, axis=-1)
    x = x + (g_a * y_a + g_b * y_b) @ w_out
    h2 = _rmsnorm(x, norm2_g)
    x = x + jnp.square(jax.nn.relu(h2 @ w_mlp1)) @ w_mlp2
    return x


def setup_inputs(seed: int = 0) -> dict:
    key = jax.random.key(seed)
    ks = jax.random.split(key, 32)

    def nrm(k, shape, scale):
        return jax.random.normal(k, shape, jnp.float32) * scale

    L = DEPTH
    return {
        "x_prompt": nrm(ks[0], (BATCH, SEQ, D_MODEL), 1.0),
        "x_sample": nrm(ks[1], (DEC_BATCH, DEC_SEQ, D_MODEL), 1.0),
        "norm1_g": 1.0 + nrm(ks[2], (L, D_MODEL), 0.02),
        "w_in": nrm(ks[3], (L, D_MODEL, D_IN), D_MODEL ** -0.5),
        "conv_w": nrm(ks[4], (L, SHORT_CONV, 3 * D_HYENA), SHORT_CONV ** -0.5),
        "conv_b": nrm(ks[5], (L, 3 * D_HYENA), 0.02),
        "filt_w1": nrm(ks[6], (L, FILTER_EMB_DIM, FILTER_HIDDEN), FILTER_EMB_DIM ** -0.5),
        "filt_b1": nrm(ks[7], (L, FILTER_HIDDEN), 0.02),
        "filt_w2": nrm(ks[8], (L, FILTER_HIDDEN, FILTER_HIDDEN), FILTER_HIDDEN ** -0.5),
        "filt_b2": nrm(ks[9], (L, FILTER_HIDDEN), 0.02),
        "filt_w3": nrm(ks[10], (L, FILTER_HIDDEN, FILTER_HIDDEN), FILTER_HIDDEN ** -0.5),
        "filt_b3": nrm(ks[11], (L, FILTER_HIDDEN), 0.02),
        "filt_w4": nrm(ks[12], (L, FILTER_HIDDEN, 2 * D_HYENA), FILTER_HIDDEN ** -0.5),
        "filt_freq": 1.0 + nrm(ks[13], (L, FILTER_HIDDEN), 0.02),
        "hyena_skip": nrm(ks[14], (L, D_HYENA), 0.1),
        "w_fnet_map": nrm(ks[15], (L, D_FNET, D_MODEL), D_FNET ** -0.5),
        "w_hyena_out": nrm(ks[16], (L, D_HYENA, D_MODEL), D_HYENA ** -0.5),
        "w_gate": nrm(ks[17], (L, D_MODEL, N_BRANCHES * D_MODEL), D_MODEL ** -0.5),
        "b_gate": nrm(ks[18], (L, N_BRANCHES * D_MODEL), 0.02),
        "w_out": nrm(ks[19], (L, D_MODEL, D_MODEL), D_MODEL ** -0.5),
        "norm2_g": 1.0 + nrm(ks[20], (L, D_MODEL), 0.02),
        "w_mlp1": nrm(ks[21], (L, D_MODEL, D_FF), D_MODEL ** -0.5),
        "w_mlp2": nrm(ks[22], (L, D_FF, D_MODEL), D_FF ** -0.5),
        "norm_f_g": 1.0 + nrm(ks[23], (D_MODEL,), 0.02),
    }


def reference(x_prompt, x_sample, norm1_g, w_in, conv_w, conv_b, filt_w1, filt_b1, filt_w2, filt_b2,
              filt_w3, filt_b3, filt_w4, filt_freq, hyena_skip, w_fnet_map, w_hyena_out, w_gate,
              b_gate, w_out, norm2_g, w_mlp1, w_mlp2, norm_f_g):
    xp = x_prompt
    xs = x_sample
    for l in range(DEPTH):
        params = (norm1_g[l], w_in[l], conv_w[l], conv_b[l], filt_w1[l], filt_b1[l], filt_w2[l],
                  filt_b2[l], filt_w3[l], filt_b3[l], filt_w4[l], filt_freq[l], hyena_skip[l],
                  w_fnet_map[l], w_hyena_out[l], w_gate[l], b_gate[l], w_out[l], norm2_g[l],
                  w_mlp1[l], w_mlp2[l])
        xp = _layer(xp, *params)
        xs = _layer(xs, *params)
    y_prompt = _rmsnorm(xp, norm_f_g)
    y_sample = _rmsnorm(xs, norm_f_g)
    return (y_prompt, y_sample)
```

```python
import math
from contextlib import ExitStack

import numpy as np
import concourse.bass as bass
import concourse.mybir as mybir
from concourse.bass_utils import run_bass_kernel_spmd

F32 = mybir.dt.float32
BF16 = mybir.dt.bfloat16
U32 = mybir.dt.uint32
AF = mybir.ActivationFunctionType
ALU = mybir.AluOpType
EPS = 1e-6


class _Op:
    __slots__ = ("eng", "fn", "deps", "is_dma", "sig", "ord", "sem", "semval", "idx")

    def __init__(self, eng, fn, is_dma):
        self.eng = eng
        self.fn = fn
        self.deps = []
        self.is_dma = is_dma
        self.sig = False
        self.ord = None
        self.sem = None
        self.semval = None


class Sched:
    STREAM = {"pe": "pe", "dve": "dve", "act": "act", "pool": "pool",
              "sp": "sp", "actq": "act", "poolq": "pool", "cc": "pool"}
    CSTREAMS = ("pe", "dve", "act", "pool")

    def __init__(self, nc, stack, n_dma_sems=52, n_eng_sems=4, epoch=15000):
        self.nc = nc
        self.epoch = epoch
        self.n_dma_sems = n_dma_sems
        self.eng_sems = {s: [stack.enter_context(nc.semaphore(f"s_{s}_{i}")) for i in range(n_eng_sems)]
                         for s in self.CSTREAMS}
        self.dma_sems = [stack.enter_context(nc.semaphore(f"s_dma_{i}")) for i in range(n_dma_sems)]
        self.cc_sems = [stack.enter_context(nc.semaphore(f"s_cc_{i}")) for i in range(8)]
        self.ncc = 0
        self.cc_used = []
        self.counts = {s: 0 for s in self.CSTREAMS}
        self.dma_cnt = [0] * n_dma_sems
        self.dma_pool = {"sp": (0, 24), "act": (24, 14), "pool": (38, 8)}
        self.dma_rr = {"sp": 0, "act": 0, "pool": 0}
        self.barrier = []
        self.n_wait = 0
        self._reset()

    def _reset(self):
        self.ops = []
        self.lastw = {}
        self.readers = {}

    def op(self, eng, fn, reads=(), writes=()):
        is_dma = eng in ("sp", "actq", "poolq", "cc")
        o = _Op(eng, fn, is_dma)
        o.idx = len(self.ops)
        stream = self.STREAM[eng]
        deps = set()
        for r in reads:
            w = self.lastw.get(r)
            if w is not None:
                deps.add(w)
        for r in writes:
            w = self.lastw.get(r)
            if w is not None:
                deps.add(w)
            for rd in self.readers.get(r, {}).values():
                deps.add(rd)
        final = []
        for d in deps:
            if not d.is_dma and self.STREAM[d.eng] == stream:
                if stream == "pe":
                    continue
            final.append(d)
            d.sig = True
        o.deps = final
        for r in writes:
            self.lastw[r] = o
            self.readers[r] = {}
        for r in reads:
            self.readers.setdefault(r, {})[stream if not is_dma else ("dma", o.idx)] = o
        self.ops.append(o)
        return o

    def emit(self, stack):
        nc = self.nc
        last = {}
        for o in self.ops:
            if o.fn is None:
                continue
            if o.is_dma:
                o.sig = True
            else:
                last[self.STREAM[o.eng]] = o
        for o in last.values():
            o.sig = True
        recycle = {}
        for o in self.ops:
            if not o.sig:
                continue
            if o.eng == "cc":
                o.sem = self.cc_sems[self.ncc]
                self.ncc += 1
                o.semval = 1
                self.cc_used.append(o.sem)
            elif o.is_dma:
                stream = self.STREAM[o.eng]
                lo, n = self.dma_pool[stream]
                k = lo + self.dma_rr[stream] % n
                self.dma_rr[stream] += 1
                if self.dma_cnt[k]:
                    recycle[o] = (self.dma_sems[k], 16 * self.dma_cnt[k])
                self.dma_cnt[k] += 1
                o.sem = self.dma_sems[k]
                o.semval = 16 * self.dma_cnt[k]
            else:
                s = self.STREAM[o.eng]
                self.counts[s] += 1
                e = (self.counts[s] - 1) // self.epoch
                o.sem = self.eng_sems[s][e]
                o.semval = self.counts[s] - e * self.epoch
        prev_barrier = self.barrier
        block = stack.enter_context(nc.Block())

        def run_stream(stream, e):
            waited = {}

            def wait(sem, val):
                if waited.get(id(sem), 0) >= val:
                    return
                e.wait_ge(sem, val)
                waited[id(sem)] = val
                self.n_wait += 1

            for sem, val in prev_barrier:
                wait(sem, val)
            for o in self.ops:
                if self.STREAM[o.eng] != stream:
                    continue
                for d in o.deps:
                    wait(d.sem, d.semval)
                if o.fn is None:
                    continue
                if o in recycle:
                    wait(*recycle[o])
                inst = o.fn(e)
                if o.sig:
                    inst.then_inc(o.sem, 16 if (o.is_dma and o.eng != "cc") else 1)

        @block.tensor
        def _(e):
            run_stream("pe", e)

        @block.vector
        def _(e):
            run_stream("dve", e)

        @block.scalar
        def _(e):
            run_stream("act", e)

        @block.gpsimd
        def _(e):
            run_stream("pool", e)

        @block.sync
        def _(e):
            run_stream("sp", e)

        bar = []
        for s in self.CSTREAMS:
            c = self.counts[s]
            if c:
                e = (c - 1) // self.epoch
                bar.append((self.eng_sems[s][e], c - e * self.epoch))
        for k in range(self.n_dma_sems):
            if self.dma_cnt[k]:
                bar.append((self.dma_sems[k], 16 * self.dma_cnt[k]))
        for sem in self.cc_used:
            bar.append((sem, 1))
        self.barrier = bar
        self._reset()


class Ring:
    def __init__(self, tiles, name):
        self.tiles = tiles
        self.name = name
        self.i = -1

    def next(self):
        self.i += 1
        k = self.i % len(self.tiles)
        return self.tiles[k], (self.name, k)


class Cfg:
    def __init__(self, D=2048, DFF=8192, LP=8192, LS=4096, TT=512):
        self.D, self.DFF, self.LP, self.LS, self.TT = D, DFF, LP, LS, TT
        self.NCORES = 8
        self.KD = D // 128
        self.NTOK = 2 * LP + 2 * LS
        self.LOCP, self.LOCS = LP // 8, LS // 8
        self.NLOC = 2 * self.LOCP + 2 * self.LOCS
        self.GD = D // 8
        assert self.GD == 256 and D // 2 == 1024
        self.seqs = [("p0", LP, 0, 0, self.LOCP), ("p1", LP, LP, self.LOCP, self.LOCP),
                     ("s0", LS, 2 * LP, 2 * self.LOCP, self.LOCS),
                     ("s1", LS, 2 * LP + LS, 2 * self.LOCP + self.LOCS, self.LOCS)]


def weight_blocks(cfg):
    D, DFF, KD = cfg.D, cfg.DFF, cfg.KD
    bl = [("gate", b) for b in range(2 * D // 256)]
    bl += [("map", b) for b in range(D // 512)] + [("hout", b) for b in range(D // 512)]
    bl += [("out", nb, kb) for nb in range(D // 512) for kb in range(KD // 8)]
    bl += [("m1", b) for b in range(DFF // 256)]
    bl += [("m2", nb, kb) for nb in range(D // 512) for kb in range((DFF // 128) // 8)]
    while len(bl) % 8:
        bl.append(("pad", len(bl)))
    return bl


def _extract_block(key, W):
    name = key[0]
    if name == "pad":
        return np.zeros((128, 4096), np.float32)
    if name in ("gate", "m1"):
        sub = W[name][:, key[1] * 256:(key[1] + 1) * 256]
    elif name in ("map", "hout"):
        sub = W[name][:, key[1] * 512:(key[1] + 1) * 512]
    else:
        nb, kb = key[1], key[2]
        sub = W[name][kb * 1024:(kb + 1) * 1024, nb * 512:(nb + 1) * 512]
    k = sub.shape[0] // 128
    return np.ascontiguousarray(sub.reshape(k, 128, sub.shape[1]).transpose(1, 0, 2)).reshape(128, -1)


def _dft_consts(L):
    out = {}
    N = 2 * L
    N1 = N // 128
    h = N1 // 2
    n1 = np.arange(h)[:, None].astype(np.float64)
    k1 = np.arange(N1)[None, :].astype(np.float64)
    C = np.cos(2 * np.pi * n1 * k1 / N1)
    S = np.sin(2 * np.pi * n1 * k1 / N1)
    out["hR1"] = np.concatenate([np.concatenate([C, -S], 1), np.concatenate([S, C], 1)], 0)
    n2 = np.arange(128)[:, None].astype(np.float64)
    ang = 2 * np.pi * n2 * k1 / N
    out["hT"] = np.concatenate([np.cos(ang), -np.sin(ang)], 1)
    k1c = np.arange(N1)[:, None].astype(np.float64)
    n2r = np.arange(128)[None, :].astype(np.float64)
    ang2 = 2 * np.pi * k1c * n2r / N
    out["hTi"] = np.concatenate([np.cos(ang2), np.sin(ang2)], 1) / N
    nn = np.arange(h)[None, :].astype(np.float64)
    C1 = np.cos(2 * np.pi * k1c * nn / N1)
    S1 = np.sin(2 * np.pi * k1c * nn / N1)
    out["hLA"] = np.concatenate([C1, S1], 1)
    out["hLB"] = np.concatenate([-S1, C1], 1)
    M1 = L // 128
    m1 = np.arange(M1)[:, None].astype(np.float64)
    q1 = np.arange(M1)[None, :].astype(np.float64)
    Cf = np.cos(2 * np.pi * m1 * q1 / M1)
    Sf = np.sin(2 * np.pi * m1 * q1 / M1)
    out["fR1"] = np.concatenate([np.concatenate([Cf, -Sf], 1), np.concatenate([Sf, Cf], 1)], 0)
    angf = 2 * np.pi * n2 * q1 / L
    sc = 1.0 / math.sqrt(L * 256.0)
    out["fT"] = np.concatenate([np.cos(angf), -np.sin(angf)], 1) * sc
    return {k: v.astype(np.float32) for k, v in out.items()}


def _common_consts():
    a = np.arange(128)[:, None].astype(np.float64)
    b = np.arange(128)[None, :].astype(np.float64)
    C2 = np.cos(2 * np.pi * a * b / 128)
    S2 = np.sin(2 * np.pi * a * b / 128)
    m = np.stack([C2, S2, -S2, -C2, C2, S2, -S2, C2], 1)
    return m.astype(np.float32)


def _filter_embed(L):
    t = np.linspace(0.0, 1.0, L, dtype=np.float32)[:, None]
    pos = np.arange(L, dtype=np.float32)[:, None]
    bands = np.linspace(1e-4, 16 - 1, 16, dtype=np.float32)[None, :]
    ang = (np.float32(2.0 * math.pi / L) * pos * bands).astype(np.float32)
    z = np.concatenate([t, np.cos(ang), -np.sin(ang)], axis=-1).astype(np.float32)
    return np.ascontiguousarray(z.T)


def _decay(L, c):
    t = np.linspace(0.0, 1.0, L, dtype=np.float32)[None, :]
    deltas = np.abs(np.linspace(math.log(1e-2) / 0.3, math.log(1e-2) / 1.5, 1024, dtype=np.float32))
    d = deltas[128 * c:128 * c + 128][:, None]
    return np.exp(-t * d).astype(np.float32)


def build(cfg, debug=False, p1_only=False, stop_after=None):
    D, DFF, TT, KD = cfg.D, cfg.DFF, cfg.TT, cfg.KD
    NS = TT // 128
    nc = bass.Bass("TRN2", target_bir_lowering=False)

    def din(name, shape, dt=F32):
        return nc.dram_tensor(name, list(shape), dt, kind="ExternalInput").ap()

    def dscr(name, shape, dt=BF16):
        return nc.dram_tensor(name, list(shape), dt).ap()

    x_all = din("x_all", [cfg.NTOK, D])
    x_loc = din("x_loc", [cfg.NLOC, D])
    g1c = din("g1c", [128, KD]); g2c = din("g2c", [128, KD]); gfb = din("gfb", [128, D])
    bgc = din("bgc", [128, 2 * KD])
    w_in_c = din("w_in_c", [D, 640])
    convc = din("convc", [128, 12])
    fw1 = din("fw1", [33, 64]); fw2 = din("fw2", [64, 64]); fw3 = din("fw3", [64, 64])
    fb = din("fb", [64, 4]); fw4c = din("fw4c", [64, 256])
    skc = din("skc", [128, 1])
    g1b = din("g1b", [128, D])
    g2b = din("g2b", [128, D])
    ident = din("ident", [128, 128])
    cdft = din("cdft", [128, 2, 2, 128])
    c2m = din("c2m", [128, 8, 128])
    idx_in = din("idx", [128, 128], U32)
    Ls = {"p": cfg.LP, "s": cfg.LS}
    consts_in = {}
    for t in ("p", "s"):
        L = Ls[t]
        N1, M1 = 2 * L // 128, L // 128
        consts_in[t] = dict(
            zT=din(f"zT_{t}", [33, L]), decT=din(f"decT_{t}", [128, L]),
            hR1=din(f"hR1_{t}", [N1, 2 * N1]), hT=din(f"hT_{t}", [128, 2 * N1]),
            hTi=din(f"hTi_{t}", [N1, 256]), hLA=din(f"hLA_{t}", [N1, N1]), hLB=din(f"hLB_{t}", [N1, N1]),
            fR1=din(f"fR1_{t}", [2 * M1, 2 * M1]), fT=din(f"fT_{t}", [128, 2 * M1]))
    y_loc = nc.dram_tensor("y_loc", [cfg.NLOC, D], F32, kind="ExternalOutput").ap()
    KH = (D // 2) // 128
    KB = 8
    blocks = weight_blocks(cfg)
    NBLK = len(blocks)
    assert NBLK % 8 == 0
    w_all = din("w_all", [NBLK * 128, 4096])
    WBL = dscr("WBL", [NBLK * 128, 4096])
    bidx = {b_: i for i, b_ in enumerate(blocks)}
    n_early = sum(1 for b_ in blocks if b_[0] in ("gate", "map", "hout", "out"))

    def wblock(key):
        g = bidx[key]
        return WBL[g * 128:(g + 1) * 128, :]

    def record_wconv(lo, hi, step=4):
        for g in range(lo, hi, step):
            g1 = min(hi, g + step)
            S.op("poolq", lambda e, g=g, g1=g1: e.dma_start(out=WBL[g * 128:g1 * 128, :], in_=w_all[g * 128:g1 * 128, :]),
                 writes=[("wconv", g)])
    scr_h = {t: dscr(f"scr_h_{t}", [3, 2 * (Ls[t] // 128), 128, 128]) for t in ("p", "s")}
    scr_z = {n: dscr(f"scr_z_{n}", [2 * (L // 128), 128, 128]) for (n, L, _, _, _) in cfg.seqs}
    scr_f = {t: dscr(f"scr_f_{t}", [2, Ls[t] // 128, 128, 128]) for t in ("p", "s")}
    snd = {n: dscr(f"snd_{n}", [256, L]) for (n, L, _, _, _) in cfg.seqs}
    gat = {n: dscr(f"gat_{n}", [8 * 256, L]) for (n, L, _, _, _) in cfg.seqs}
    dbg = {}
    if debug:
        for (n, L, _, _, _) in cfg.seqs:
            dbg[n] = nc.dram_tensor(f"dbg_snd_{n}", [256, L], BF16, kind="ExternalOutput").ap()

    with ExitStack() as top:
        S = Sched(nc, top)
        ccsem = top.enter_context(nc.semaphore("ccsem"))
        uid = [0]

        def sb(name, shape, dt=F32, st=None):
            uid[0] += 1
            return (st or top).enter_context(nc.sbuf_tensor(f"{name}_{uid[0]}", list(shape), dt))

        def ps_(st, name, shape, dt=F32):
            uid[0] += 1
            return st.enter_context(nc.psum_tensor(f"{name}_{uid[0]}", list(shape), dt))
        idf = sb("idf", [128, 128]); idb = sb("idb", [128, 128], BF16)
        onesf = sb("onesf", [128, 128])
        g1t = sb("g1t", [128, KD]); g2t = sb("g2t", [128, KD]); bgt = sb("bgt", [128, 2 * KD])
        epsc = sb("epsc", [128, 1]); hpic = sb("hpic", [128, 1])
        c2b = sb("c2b", [128, 8, 128], BF16)
        idxt = sb("idxt", [128, 128], U32)
        S.op("sp", lambda e: e.dma_start(out=idf[:], in_=ident), writes=["idf"])
        S.op("sp", lambda e: e.dma_start(out=g1t[:], in_=g1c), writes=["g1t"])
        S.op("sp", lambda e: e.dma_start(out=g2t[:], in_=g2c), writes=["g2t"])
        S.op("sp", lambda e: e.dma_start(out=bgt[:], in_=bgc), writes=["bgt"])
        S.op("sp", lambda e: e.dma_start(out=idxt[:], in_=idx_in), writes=["idxt"])
        S.op("poolq", lambda e: e.dma_start(out=c2b[:], in_=c2m), writes=["c2b"])
        S.op("dve", lambda e: e.tensor_copy(out=idb[:], in_=idf[:]), reads=["idf"], writes=["idb"])
        S.op("pool", lambda e: e.memset(epsc[:], EPS), writes=["epsc"])
        S.op("pool", lambda e: e.memset(hpic[:], math.pi / 2), writes=["hpic"])
        S.op("pool", lambda e: e.memset(onesf[:], 1.0), writes=["onesf"])

        def norm_transpose(PT, xt, xkey, gt, gkey, hT, hkey, hb_ring, st_ring, junk, extra_reads=()):
            for s in range(NS):
                stt, stk = st_ring.next()
                hb, hbk = hb_ring.next()
                S.op("act", lambda e, s=s, stt=stt: e.activation(out=junk[:], in_=xt[:, s, :], func=AF.Square,
                                                                  accum_out=stt[:, 0:1]),
                     reads=[xkey] + list(extra_reads), writes=["junk", stk])
                S.op("act", lambda e, stt=stt: e.activation(out=stt[:, 1:2], in_=stt[:, 0:1], func=AF.Sqrt,
                                                            scale=1.0 / D, bias=epsc[:, 0:1]),
                     reads=[stk, "epsc"], writes=[(stk, "b")])
                S.op("dve", lambda e, stt=stt: e.reciprocal(out=stt[:, 2:3], in_=stt[:, 1:2]),
                     reads=[(stk, "b")], writes=[(stk, "c")])
                S.op("act", lambda e, s=s, stt=stt, hb=hb: e.activation(out=hb[:], in_=xt[:, s, :], func=AF.Copy,
                                                                         scale=stt[:, 2:3]),
                     reads=[xkey, (stk, "c")], writes=[hbk])
                for k4 in range(KD // 4):
                    for j in range(4):
                        k = k4 * 4 + j
                        S.op("pe", lambda e, k=k, j=j, hb=hb: e.transpose(out=PT[:, j * 128:(j + 1) * 128],
                                                                         in_=hb[:, k * 128:(k + 1) * 128],
                                                                         identity=idb[:]),
                             reads=[hbk, "idb"], writes=["PT"])
                    for j in range(4):
                        k = k4 * 4 + j
                        S.op("act", lambda e, k=k, j=j, s=s: e.activation(
                            out=hT[:, k, s * 128:(s + 1) * 128], in_=PT[:, j * 128:(j + 1) * 128],
                            func=AF.Copy, scale=gt[:, k:k + 1]),
                            reads=[gkey], writes=["PT", (hkey, k)])
            return stt, stk

        pt_i = [0]

        def norm_wide(PTs, xt, xkey, gbt, gbk, hT, hkey, hb_ring, st_ring, extra_reads=()):
            stt, stk = st_ring.next()
            hbs = []
            for s_ in range(NS):
                hb, hbk = hb_ring.next()
                hbs.append((hb, hbk))
                S.op("act", lambda e, s_=s_, stt=stt, hb=hb: e.activation(out=hb[:], in_=xt[:, s_, :], func=AF.Square,
                                                                           accum_out=stt[:, s_:s_ + 1]),
                     reads=[xkey] + list(extra_reads), writes=[hbk, (stk, "a", s_)])
            S.op("act", lambda e, stt=stt: e.activation(out=stt[:, NS:2 * NS], in_=stt[:, 0:NS], func=AF.Sqrt,
                                                        scale=1.0 / D, bias=epsc[:, 0:1]),
                 reads=[(stk, "a", s_) for s_ in range(NS)] + ["epsc"], writes=[(stk, "b")])
            S.op("dve", lambda e, stt=stt: e.reciprocal(out=stt[:, 2 * NS:3 * NS], in_=stt[:, NS:2 * NS]),
                 reads=[(stk, "b")], writes=[(stk, "c")])
            for s_ in range(NS):
                hb, hbk = hbs[s_]
                S.op("dve", lambda e, s_=s_, stt=stt, hb=hb: e.scalar_tensor_tensor(
                    out=hb[:], in0=xt[:, s_, :], scalar=stt[:, 2 * NS + s_:2 * NS + s_ + 1], in1=gbt[:],
                    op0=ALU.mult, op1=ALU.mult), reads=[xkey, (stk, "c"), gbk], writes=[hbk])
                for k4 in range(KD // 4):
                    pt_i[0] += 1
                    pi = pt_i[0] % 2
                    PT, ptk = PTs[pi], ("PT", pi)
                    for j in range(4):
                        k = k4 * 4 + j
                        S.op("pe", lambda e, k=k, j=j, hb=hb, PT=PT: e.transpose(out=PT[:, j * 128:(j + 1) * 128],
                                                                                in_=hb[:, k * 128:(k + 1) * 128], identity=idb[:]),
                             reads=[hbk, "idb"], writes=[ptk])
                    dst = hT[:, k4 * 4:k4 * 4 + 4, s_ * 128:(s_ + 1) * 128]
                    src = PT[:, 0:512].rearrange("p (j f) -> p j f", j=4)
                    if pi == 0:
                        S.op("act", lambda e, dst=dst, src=src: e.activation(out=dst, in_=src, func=AF.Copy),
                             writes=[ptk] + [(hkey, k4 * 4 + j) for j in range(4)])
                    else:
                        S.op("dve", lambda e, dst=dst, src=src: e.tensor_copy(out=dst, in_=src),
                             writes=[ptk] + [(hkey, k4 * 4 + j) for j in range(4)])

        with ExitStack() as ph:
            sbp = lambda name, shape, dt=F32: sb(name, shape, dt, ph)
            winb = sbp("winb", [128, KD, 640], BF16)
            S.op("poolq", lambda e: e.dma_start(out=winb[:], in_=w_in_c.rearrange("(k p) n -> p k n", p=128)),
                 writes=["winb"])
            cdb = sbp("cdb", [128, 2, 2, 128], BF16)
            S.op("poolq", lambda e: e.dma_start(out=cdb[:], in_=cdft), writes=["cdb"])
            cvt = sbp("cvt", [128, 12])
            S.op("sp", lambda e: e.dma_start(out=cvt[:], in_=convc), writes=["cvt"])
            skct = sbp("skct", [128, 1])
            S.op("sp", lambda e: e.dma_start(out=skct[:], in_=skc), writes=["skct"])
            g1bt = sbp("g1bt", [128, D])
            S.op("sp", lambda e: e.dma_start(out=g1bt[:], in_=g1b), writes=["g1bt"])
            PA = ps_(ph, "PA", [128, 1024])
            PTs = [ps_(ph, f"PT{i}", [128, 1024], BF16) for i in range(2)]
            record_wconv(0, n_early)
            x_ring = Ring([sbp(f"xt{i}", [128, NS, D]) for i in range(2)], "xt")
            hb_ring = Ring([sbp(f"hb{i}", [128, D], BF16) for i in range(4)], "hb")
            st_ring = Ring([sbp(f"st{i}", [128, 3 * NS]) for i in range(2)], "st")
            hT_ring = Ring([sbp(f"hT{i}", [128, KD, TT], BF16) for i in range(2)], "hT")
            AT_ring = Ring([sbp(f"AT{i}", [128, 2, TT], BF16) for i in range(2)], "AT")
            ZT_ring = Ring([sbp(f"ZT{i}", [128, 2, TT], BF16) for i in range(2)], "ZT")
            raw = [sbp(f"raw{i}", [128, 3, TT + 2]) for i in range(2)]
            asm = [sbp(f"asm{i}", [128, 3, TT]) for i in range(2)]
            tmpc = sbp("tmpc", [128, TT])
            ob_ring = Ring([sbp(f"ob{i}", [128, 3, TT], BF16) for i in range(2)], "ob")
            pbank = [("PA", 0), ("PA", 1)]
            pb_i = [0]

            def pnext():
                pb_i[0] += 1
                b = pb_i[0] % 2
                return PA[:, b * 512:b * 512 + TT], ("PA", b)

            for (sname, L, xoff, _, _) in cfg.seqs:
                t = sname[0]
                sidx = int(sname[1])
                R = L // 128
                ntile = L // TT
                for i in range(ntile + 1):
                    cur, prv = raw[i % 2], raw[(i + 1) % 2]
                    ck, pk = ("raw", i % 2), ("raw", (i + 1) % 2)
                    if i == 0:
                        S.op("dve", lambda e, cur=cur: e.memset(cur[:, :, 0:2], 0.0), writes=[ck])
                    else:
                        S.op("dve", lambda e, cur=cur, prv=prv: e.tensor_copy(out=cur[:, :, 0:2], in_=prv[:, :, TT:TT + 2]),
                             reads=[pk], writes=[ck])
                    if i < ntile:
                        xt, xk = x_ring.next()
                        t0 = xoff + i * TT
                        S.op("sp", lambda e, xt=xt, t0=t0: e.dma_start(
                            out=xt[:], in_=x_all[t0:t0 + TT, :].rearrange("(s p) d -> p s d", p=128)), writes=[xk])
                        hT, hk = hT_ring.next()
                        norm_wide(PTs, xt, xk, g1bt, "g1bt", hT, hk, hb_ring, st_ring)
                        hreads = [(hk, k) for k in range(KD)]
                        AT, ak = AT_ring.next()
                        for m in range(2):
                            ps, pk_ = pnext()
                            for k in range(KD):
                                S.op("pe", lambda e, k=k, m=m, ps=ps, hT=hT: e.matmul(
                                    ps, lhsT=winb[:, k, m * 128:(m + 1) * 128], rhs=hT[:, k, :],
                                    start=(k == 0), stop=(k == KD - 1)), reads=hreads + ["winb"], writes=[pk_])
                            S.op("act", lambda e, m=m, ps=ps, AT=AT: e.activation(out=AT[:, m, :], in_=ps, func=AF.Copy),
                                 writes=[pk_, (ak, m)])
                        ZT, zk = ZT_ring.next()
                        for comp in range(2):
                            ps, pk_ = pnext()
                            for m in range(2):
                                S.op("pe", lambda e, m=m, comp=comp, ps=ps, AT=AT: e.matmul(
                                    ps, lhsT=cdb[:, m, comp, :], rhs=AT[:, m, :], start=(m == 0), stop=(m == 1)),
                                    reads=[(ak, 0), (ak, 1), "cdb"], writes=[pk_])
                            S.op("dve", lambda e, comp=comp, ps=ps, ZT=ZT: e.tensor_copy(out=ZT[:, comp, :], in_=ps),
                                 writes=[pk_, (zk, comp)])
                        r0 = i * NS
                        for comp in range(2):
                            S.op("actq", lambda e, ZT=ZT, r0=r0, sname=sname, R=R, comp=comp: e.dma_start(
                                out=scr_z[sname][comp * R + r0:comp * R + r0 + NS, :, :].rearrange("r ch f -> ch r f"),
                                in_=ZT[:, comp, :].rearrange("p (r f) -> p r f", f=128)),
                                reads=[(zk, comp)], writes=[("scr_z", sname, comp)])
                        for q in range(3):
                            ps, pk_ = pnext()
                            for k in range(KD):
                                S.op("pe", lambda e, k=k, q=q, ps=ps, hT=hT: e.matmul(
                                    ps, lhsT=winb[:, k, 256 + q * 128:256 + (q + 1) * 128], rhs=hT[:, k, :],
                                    start=(k == 0), stop=(k == KD - 1)), reads=hreads + ["winb"], writes=[pk_])
                            S.op("act", lambda e, q=q, ps=ps, cur=cur: e.activation(out=cur[:, q, 2:TT + 2], in_=ps,
                                                                                    func=AF.Copy),
                                 writes=[pk_, ck])
                    else:
                        S.op("dve", lambda e, cur=cur: e.memset(cur[:, :, 2:TT + 2], 0.0), writes=[ck])
                    a_cur, a_prv = asm[i % 2], asm[(i + 1) % 2]
                    ack, apk = ("asm", i % 2), ("asm", (i + 1) % 2)
                    for q in range(3):
                        w0, w1, w2, bb = (cvt[:, 4 * q + j:4 * q + j + 1] for j in range(4))
                        pieces = []
                        if i < ntile:
                            pieces.append((1, TT, a_cur, 0, ack))
                        if i > 0:
                            pieces.append((0, 1, a_prv, TT - 1, apk))
                        for (j0, j1, dst, d0, dk) in pieces:
                            n = j1 - j0
                            S.op("dve", lambda e, q=q, j0=j0, n=n, cur=cur, w0=w0, bb=bb: e.tensor_scalar(
                                out=tmpc[:, 0:n], in0=cur[:, q, j0:j0 + n], scalar1=w0, scalar2=bb,
                                op0=ALU.mult, op1=ALU.add), reads=[ck, "cvt"], writes=["tmpc"])
                            S.op("dve", lambda e, q=q, j0=j0, n=n, cur=cur, w1=w1: e.scalar_tensor_tensor(
                                out=tmpc[:, 0:n], in0=cur[:, q, j0 + 1:j0 + 1 + n], scalar=w1, in1=tmpc[:, 0:n],
                                op0=ALU.mult, op1=ALU.add), reads=[ck, "tmpc"], writes=["tmpc"])
                            S.op("dve", lambda e, q=q, j0=j0, n=n, cur=cur, w2=w2, dst=dst, d0=d0: e.scalar_tensor_tensor(
                                out=dst[:, q, d0:d0 + n], in0=cur[:, q, j0 + 2:j0 + 2 + n], scalar=w2, in1=tmpc[:, 0:n],
                                op0=ALU.mult, op1=ALU.add), reads=[ck, "tmpc"], writes=[dk])
                    if i > 0:
                        ob, obk = ob_ring.next()
                        S.op("act", lambda e, ob=ob, a=a_prv: e.activation(out=ob[:, 0, :], in_=a[:, 0, :], func=AF.Copy),
                             reads=[apk], writes=[(obk, 0)])
                        S.op("dve", lambda e, ob=ob, a=a_prv: e.scalar_tensor_tensor(out=ob[:, 1, :], in0=a[:, 2, :], scalar=skct[:, 0:1],
                                                                                     in1=a[:, 1, :], op0=ALU.mult, op1=ALU.mult),
                             reads=[apk, "skct"], writes=[(obk, 1)])
                        S.op("dve", lambda e, ob=ob, a=a_prv: e.tensor_tensor(out=ob[:, 2, :], in0=a[:, 2, :], in1=a[:, 1, :],
                                                                              op=ALU.mult),
                             reads=[apk], writes=[(obk, 2)])
                        r0 = sidx * R + (i - 1) * NS
                        for c3 in range(3):
                            S.op("sp", lambda e, ob=ob, r0=r0, t=t, c3=c3: e.dma_start(
                                out=scr_h[t][c3, r0:r0 + NS, :, :].rearrange("r ch f -> ch r f"),
                                in_=ob[:, c3, :].rearrange("p (r f) -> p r f", f=128)),
                                reads=[(obk, c3)], writes=[("scr_h", t, c3)])
            S.emit(ph)
        if stop_after == "1a":
            return nc

        with ExitStack() as ph:
            sbp = lambda name, shape, dt=F32: sb(name, shape, dt, ph)
            PE_ = ps_(ph, "PE_", [128, 512])
            w1t = sbp("w1t", [33, 64]); w2t = sbp("w2t", [64, 64]); w3t = sbp("w3t", [64, 64])
            fbt = sbp("fbt", [64, 4]); w4t = sbp("w4t", [64, 256])
            sct = sbp("sct", [64, 8])
            for (tl, src) in ((w1t, fw1), (w2t, fw2), (w3t, fw3), (fbt, fb), (w4t, fw4c)):
                S.op("sp", lambda e, tl=tl, src=src: e.dma_start(out=tl[:], in_=src), writes=[id(tl)])
            S.op("dve", lambda e: e.tensor_scalar(out=sct[:, 3:4], in0=fbt[:, 3:4], scalar1=0.5, scalar2=None, op0=ALU.mult),
                 reads=[id(fbt)], writes=["sct3"])
            S.op("dve", lambda e: e.tensor_scalar(out=sct[:, 0:3], in0=fbt[:, 0:3], scalar1=sct[:, 3:4], scalar2=None,
                                                  op0=ALU.mult), reads=[id(fbt), "sct3"], writes=["sct"])
            FT = 512
            zt_ring = Ring([sbp(f"zt{i}", [33, FT]) for i in range(2)], "zt")
            dec_ring = Ring([sbp(f"dec{i}", [128, FT]) for i in range(2)], "dec")
            h_tiles = [sbp(f"hm{i}", [64, FT]) for i in range(3)]
            s1 = sbp("fs1", [64, FT]); s2 = sbp("fs2", [64, FT])
            kf_all = sbp("kf_all", [128, 2, cfg.LP])
            kb_ring = Ring([sbp(f"kb{i}", [128, 2, FT], BF16) for i in range(2)], "kb")
            junkf = sbp("junkf", [128, FT])

            def filt(t):
                L = Ls[t]
                nt = L // FT
                ssq = sbp(f"ssq_{t}", [128, 2 * nt + 4])
                S.op("pool", lambda e: e.memset(ssq[:], 0.0), writes=[("ssq", t)])
                for i in range(nt):
                    zt, zk = zt_ring.next()
                    S.op("sp", lambda e, zt=zt, i=i: e.dma_start(out=zt[:], in_=consts_in[t]["zT"][:, i * FT:(i + 1) * FT]), writes=[zk])
                    dec, dk = dec_ring.next()
                    S.op("sp", lambda e, dec=dec, i=i: e.dma_start(out=dec[:], in_=consts_in[t]["decT"][:, i * FT:(i + 1) * FT]), writes=[dk])
                    prev, prevk, kdim = zt, zk, 33
                    for layer, wt in enumerate((w1t, w2t, w3t)):
                        S.op("pe", lambda e, wt=wt, prev=prev, kdim=kdim: e.matmul(PE_[0:64, :], lhsT=wt[0:kdim, :], rhs=prev[0:kdim, :],
                                                                                   start=True, stop=True),
                             reads=[prevk, id(wt)], writes=["PE_"])
                        hm = h_tiles[layer]
                        S.op("act", lambda e, layer=layer: e.activation(out=s1[:], in_=PE_[0:64, :], func=AF.Sin,
                                                                       scale=sct[:, 3:4], bias=sct[:, layer:layer + 1]),
                             reads=["sct", "sct3"], writes=["PE_", "fs1"])
                        S.op("act", lambda e, layer=layer: e.activation(out=s2[:], in_=PE_[0:64, :], func=AF.Abs,
                                                                       scale=sct[:, 3:4], bias=sct[:, layer:layer + 1]),
                             reads=["sct", "sct3"], writes=["PE_", "fs2"])
                        S.op("act", lambda e: e.activation(out=s2[:], in_=s2[:], func=AF.Sin, scale=-1.0, bias=hpic[0:64, 0:1]),
                             reads=["fs2", "hpic"], writes=["fs2"])
                        S.op("dve", lambda e, hm=hm: e.scalar_tensor_tensor(out=hm[:], in0=s1[:], scalar=2.0, in1=s2[:],
                                                                            op0=ALU.mult, op1=ALU.mult),
                             reads=["fs1", "fs2"], writes=[("hm", layer)])
                        prev, prevk, kdim = hm, ("hm", layer), 64
                    for d in range(2):
                        kfs = kf_all[:, d, i * FT:(i + 1) * FT]
                        S.op("pe", lambda e, d=d, prev=prev: e.matmul(PE_[:, :], lhsT=w4t[:, d * 128:(d + 1) * 128], rhs=prev[:],
                                                                      start=True, stop=True),
                             reads=[prevk, id(w4t)], writes=["PE_"])
                        S.op("dve", lambda e, kfs=kfs, dec=dec: e.tensor_tensor(out=kfs, in0=PE_[:, :], in1=dec[:], op=ALU.mult),
                             reads=[dk], writes=["PE_", ("kf", d, i)])
                        if d == 1 and i == 0:
                            S.op("dve", lambda e: e.memset(kf_all[:, 1, 0:1], 0.0), writes=[("kf", 1, 0)])
                        S.op("act", lambda e, d=d, kfs=kfs, i=i: e.activation(out=junkf[:], in_=kfs, func=AF.Square,
                                                                              accum_out=ssq[:, 2 * i + d:2 * i + d + 1]),
                             reads=[("kf", d, i), ("ssq", t)], writes=["junkf", ("ssqc", t, i, d)])
                allss = [("ssqc", t, i, d) for i in range(nt) for d in range(2)]
                S.op("dve", lambda e: e.reduce_sum(out=ssq[:, 2 * nt:2 * nt + 1], in_=ssq[:, 0:2 * nt], axis=mybir.AxisListType.X),
                     reads=allss + [("ssq", t)], writes=[("ssqs", t)])
                S.op("act", lambda e: e.activation(out=ssq[:, 2 * nt + 1:2 * nt + 2], in_=ssq[:, 2 * nt:2 * nt + 1],
                                                   func=AF.Sqrt, scale=1.0, bias=epsc[:, 0:1]),
                     reads=[("ssqs", t), "epsc"], writes=[("ssqr", t)])
                S.op("dve", lambda e: e.reciprocal(out=ssq[:, 2 * nt + 2:2 * nt + 3], in_=ssq[:, 2 * nt + 1:2 * nt + 2]),
                     reads=[("ssqr", t)], writes=[("ssqi", t)])
                for i in range(nt):
                    kb, kbk = kb_ring.next()
                    for d in range(2):
                        eng = "act" if d == 0 else "dve"
                        if d == 0:
                            S.op("act", lambda e, kb=kb, i=i: e.activation(out=kb[:, 0, :], in_=kf_all[:, 0, i * FT:(i + 1) * FT], func=AF.Copy,
                                                                           scale=ssq[:, 2 * nt + 2:2 * nt + 3]),
                                 reads=[("kf", 0, i), ("ssqi", t)], writes=[(kbk, 0)])
                        else:
                            S.op("dve", lambda e, kb=kb, i=i: e.tensor_scalar(out=kb[:, 1, :], in0=kf_all[:, 1, i * FT:(i + 1) * FT],
                                                                              scalar1=ssq[:, 2 * nt + 2:2 * nt + 3], scalar2=None, op0=ALU.mult),
                                 reads=[("kf", 1, i), ("ssqi", t)], writes=[(kbk, 1)])
                        r0 = i * (FT // 128)
                        S.op("actq", lambda e, kb=kb, r0=r0, d=d: e.dma_start(
                            out=scr_f[t][d, r0:r0 + FT // 128, :, :].rearrange("r ch f -> ch r f"),
                            in_=kb[:, d, :].rearrange("p (r f) -> p r f", f=128)),
                            reads=[(kbk, d)], writes=[("scr_f", t, d)])

            for t in ("p", "s"):
                filt(t)
            S.emit(ph)
        if stop_after == "1b":
            return nc

        def rep_load(dst, src, G, W, swap=False, parts=128):
            h = W // 2
            for g in range(G):
                if not swap:
                    S.op("sp", lambda e, g=g: e.dma_start(out=dst[:, g, :], in_=src), writes=[(id(dst), g)])
                else:
                    S.op("sp", lambda e, g=g: e.dma_start(out=dst[:, g, 0:h], in_=src[:, h:W]), writes=[(id(dst), g, 0)])
                    S.op("sp", lambda e, g=g: e.dma_start(out=dst[:, g, h:W], in_=src[:, 0:h]), writes=[(id(dst), g, 1)])

        def bsplit(G, w):
            per = max(1, 512 // w)
            return [(a_, min(G, a_ + per)) for a_ in range(0, G, per)]

        def rep_keys(dst, G, swap):
            return [(id(dst), g) for g in range(G)] if not swap else [(id(dst), g, j) for g in range(G) for j in range(2)]

        def hyena_fft(t):
            with ExitStack() as ph:
                sbp = lambda name, shape, dt=F32: sb(name, shape, dt, ph)
                PA = ps_(ph, "PA", [128, 1024]); PB = ps_(ph, "PB", [128, 1024])
                PC = ps_(ph, "PC", [128, 512]); PD = ps_(ph, "PD", [128, 512]); PE_ = ps_(ph, "PE_", [128, 512])
                L = Ls[t]
                N1 = 2 * L // 128
                H = N1 // 2
                G = 4
                ci = consts_in[t]
                R1 = sbp("R1", [N1, 2 * N1], BF16)
                LA = sbp("LA", [N1, N1], BF16); LB = sbp("LB", [N1, N1], BF16)
                S.op("poolq", lambda e: e.dma_start(out=R1[:], in_=ci["hR1"]), writes=["R1"])
                S.op("poolq", lambda e: e.dma_start(out=LA[:], in_=ci["hLA"]), writes=["LA"])
                S.op("poolq", lambda e: e.dma_start(out=LB[:], in_=ci["hLB"]), writes=["LB"])
                TG = sbp("TG", [128, G, 2 * N1]); TGs = sbp("TGs", [128, G, 2 * N1])
                TiG = sbp("TiG", [N1, G, 256]); TiGs = sbp("TiGs", [N1, G, 256])
                rep_load(TG, ci["hT"], G, 2 * N1); rep_load(TGs, ci["hT"], G, 2 * N1, swap=True)
                rep_load(TiG, ci["hTi"], G, 256); rep_load(TiGs, ci["hTi"], G, 256, swap=True)
                n_m1 = sum(1 for b_ in blocks if b_[0] == "m1")
                if t == "p":
                    record_wconv(n_early, n_early + n_m1)
                else:
                    record_wconv(n_early + n_m1, NBLK)
                kTG, kTGs = rep_keys(TG, G, False), rep_keys(TGs, G, True)
                kTiG, kTiGs = rep_keys(TiG, G, False), rep_keys(TiGs, G, True)
                mk = lambda nm, shape, dt=F32, n=2: Ring([sbp(f"{nm}{i}", shape, dt) for i in range(n)], nm)
                in_ring = mk("hin", [N1, 3, G, 128], BF16); fl_ring = mk("hfl", [H, 2, G, 128], BF16)
                ta_r = mk("ta", [128, G, 2 * N1]); tb_r = mk("tb", [128, G, 2 * N1])
                Y_r = mk("Yx", [128, 2, G, N1], BF16, n=4)
                Kf_r = mk("Kf", [128, 2, G * N1]); Xf_r = mk("Xf", [128, 2, G, N1], BF16)
                tc_r = mk("tc", [128, G * N1], n=4)
                ua_r = mk("ua", [N1, G, 256]); ub_r = mk("ub", [N1, G, 256])
                Y2_r = mk("Y2", [N1, 2, G, 128], BF16); tv_r = mk("tv", [N1, G * 128]); yo_r = mk("yo", [N1, G, 128], BF16)
                P1 = PA[:, 0:G * 2 * N1].rearrange("p (g w) -> p g w", g=G)
                P3 = PB[0:N1, 0:G * 256].rearrange("p (g w) -> p g w", g=G)
                P4 = PE_[0:N1, 0:G * 128]
                PAk, PBk = ["PA0", "PA1"], ["PB0", "PB1"]

                def twiddle(Yout, yk):
                    ta, tak = ta_r.next(); tb, tbk = tb_r.next()
                    for (a_, b_) in bsplit(G, 2 * N1):
                        S.op("dve", lambda e, a_=a_, b_=b_: e.tensor_tensor(out=ta[:, a_:b_, :], in0=P1[:, a_:b_, :], in1=TG[:, a_:b_, :], op=ALU.mult), reads=kTG, writes=PAk + [tak])
                        S.op("dve", lambda e, a_=a_, b_=b_: e.tensor_tensor(out=tb[:, a_:b_, :], in0=P1[:, a_:b_, :], in1=TGs[:, a_:b_, :], op=ALU.mult), reads=kTGs, writes=PAk + [tbk])
                    S.op("pool", lambda e: e.tensor_tensor(out=Yout[:, 0, :, :], in0=ta[:, :, 0:N1], in1=ta[:, :, N1:2 * N1], op=ALU.subtract),
                         reads=[tak], writes=[(yk, 0)])
                    S.op("pool", lambda e: e.tensor_tensor(out=Yout[:, 1, :, :], in0=tb[:, :, 0:N1], in1=tb[:, :, N1:2 * N1], op=ALU.add),
                         reads=[tbk], writes=[(yk, 1)])

                def stage2(Y, yk, first, last, conj=False):
                    r_ = Y[:, 0, :, :].rearrange("p g k -> p (g k)")
                    i_ = Y[:, 1, :, :].rearrange("p g k -> p (g k)")
                    ykeys = [(yk, 0), (yk, 1)]
                    ci_, si_ = (3, 1) if conj else (0, 2)
                    S.op("pe", lambda e: e.matmul(PC[:, 0:G * N1], lhsT=c2b[:, 0, :], rhs=r_, start=first, stop=False), reads=ykeys + ["c2b"], writes=["PC"])
                    S.op("pe", lambda e: e.matmul(PC[:, 0:G * N1], lhsT=c2b[:, 1, :], rhs=i_, start=False, stop=last), reads=ykeys + ["c2b"], writes=["PC"])
                    S.op("pe", lambda e: e.matmul(PD[:, 0:G * N1], lhsT=c2b[:, ci_, :], rhs=i_, start=first, stop=False), reads=ykeys + ["c2b"], writes=["PD"])
                    S.op("pe", lambda e: e.matmul(PD[:, 0:G * N1], lhsT=c2b[:, si_, :], rhs=r_, start=False, stop=last), reads=ykeys + ["c2b"], writes=["PD"])

                pair = [n for (n, _, _, _, _) in cfg.seqs if n[0] == t]

                def group(g0):
                    hin, hk = in_ring.next()
                    hfl, fk = fl_ring.next()
                    for c3 in range(3):
                        S.op("sp", lambda e, c3=c3: e.dma_start(out=hin[:, c3, :, :], in_=scr_h[t][c3, :, g0:g0 + G, :]), writes=[(hk, c3)])
                    for d in range(2):
                        S.op("sp", lambda e, d=d: e.dma_start(out=hfl[:, d, :, :], in_=scr_f[t][d, :, g0:g0 + G, :]), writes=[(fk, d)])
                    Ys = []
                    for d in range(2):
                        for g in range(G):
                            S.op("pe", lambda e, g=g, d=d: e.matmul(P1[:, g, :], lhsT=hfl[:, d, g, :], rhs=R1[0:H, :], start=True, stop=True),
                                 reads=[(fk, d), "R1"], writes=PAk)
                        Yd, ydk = Y_r.next()
                        twiddle(Yd, ydk)
                        Ys.append((Yd, ydk))
                    stage2(Ys[0][0], Ys[0][1], True, False)
                    stage2(Ys[1][0], Ys[1][1], False, True, conj=True)
                    Kf, kfk = Kf_r.next()
                    S.op("act", lambda e: e.activation(out=Kf[:, 0, :], in_=PC[:, 0:G * N1], func=AF.Copy), writes=["PC", (kfk, 0)])
                    S.op("act", lambda e: e.activation(out=Kf[:, 1, :], in_=PD[:, 0:G * N1], func=AF.Copy), writes=["PD", (kfk, 1)])
                    for g in range(G):
                        S.op("pe", lambda e, g=g: e.matmul(P1[:, g, :], lhsT=hin[:, 2, g, :], rhs=R1[:, :], start=True, stop=True),
                             reads=[(hk, 2), "R1"], writes=PAk)
                    Yp, ypk = Y_r.next()
                    twiddle(Yp, ypk)
                    stage2(Yp, ypk, True, True)
                    Xf, xfk = Xf_r.next()
                    tcs = [tc_r.next() for _ in range(4)]
                    kk = [(kfk, 0), (kfk, 1)]
                    S.op("dve", lambda e: e.tensor_tensor(out=tcs[0][0][:], in0=PC[:, 0:G * N1], in1=Kf[:, 0, :], op=ALU.mult), reads=kk, writes=["PC", tcs[0][1]])
                    S.op("dve", lambda e: e.tensor_tensor(out=tcs[1][0][:], in0=PD[:, 0:G * N1], in1=Kf[:, 1, :], op=ALU.mult), reads=kk, writes=["PD", tcs[1][1]])
                    S.op("dve", lambda e: e.tensor_tensor(out=tcs[2][0][:], in0=PC[:, 0:G * N1], in1=Kf[:, 1, :], op=ALU.mult), reads=kk, writes=["PC", tcs[2][1]])
                    S.op("dve", lambda e: e.tensor_tensor(out=tcs[3][0][:], in0=PD[:, 0:G * N1], in1=Kf[:, 0, :], op=ALU.mult), reads=kk, writes=["PD", tcs[3][1]])
                    S.op("pool", lambda e: e.tensor_tensor(out=Xf[:, 0, :, :].rearrange("p g k -> p (g k)"), in0=tcs[0][0][:], in1=tcs[1][0][:], op=ALU.subtract),
                         reads=[tcs[0][1], tcs[1][1]], writes=[(xfk, 0)])
                    S.op("pool", lambda e: e.tensor_tensor(out=Xf[:, 1, :, :].rearrange("p g k -> p (g k)"), in0=tcs[2][0][:], in1=tcs[3][0][:], op=ALU.add),
                         reads=[tcs[2][1], tcs[3][1]], writes=[(xfk, 1)])
                    for g in range(G):
                        S.op("pe", lambda e, g=g: e.matmul(P3[:, g, :], lhsT=Xf[:, 0, g, :], rhs=c2b[:, 4:6, :].rearrange("p a k -> p (a k)"), start=True, stop=False),
                             reads=[(xfk, 0), (xfk, 1), "c2b"], writes=PBk)
                        S.op("pe", lambda e, g=g: e.matmul(P3[:, g, :], lhsT=Xf[:, 1, g, :], rhs=c2b[:, 6:8, :].rearrange("p a k -> p (a k)"), start=False, stop=True),
                             reads=[(xfk, 0), (xfk, 1), "c2b"], writes=PBk)
                    ua, uak = ua_r.next(); ub, ubk = ub_r.next(); Y2, y2k = Y2_r.next()
                    for (a_, b_) in bsplit(G, 256):
                        S.op("dve", lambda e, a_=a_, b_=b_: e.tensor_tensor(out=ua[:, a_:b_, :], in0=P3[:, a_:b_, :], in1=TiG[:, a_:b_, :], op=ALU.mult), reads=kTiG, writes=PBk + [uak])
                        S.op("dve", lambda e, a_=a_, b_=b_: e.tensor_tensor(out=ub[:, a_:b_, :], in0=P3[:, a_:b_, :], in1=TiGs[:, a_:b_, :], op=ALU.mult), reads=kTiGs, writes=PBk + [ubk])
                    S.op("pool", lambda e: e.tensor_tensor(out=Y2[:, 0, :, :], in0=ua[:, :, 0:128], in1=ua[:, :, 128:256], op=ALU.subtract), reads=[uak], writes=[(y2k, 0)])
                    S.op("pool", lambda e: e.tensor_tensor(out=Y2[:, 1, :, :], in0=ub[:, :, 0:128], in1=ub[:, :, 128:256], op=ALU.add), reads=[ubk], writes=[(y2k, 1)])
                    S.op("pe", lambda e: e.matmul(P4, lhsT=LA[:, :], rhs=Y2[:, 0, :, :].rearrange("p g f -> p (g f)"), start=True, stop=False),
                         reads=[(y2k, 0), (y2k, 1), "LA"], writes=["PE_"])
                    S.op("pe", lambda e: e.matmul(P4, lhsT=LB[:, :], rhs=Y2[:, 1, :, :].rearrange("p g f -> p (g f)"), start=False, stop=True),
                         reads=[(y2k, 0), (y2k, 1), "LB"], writes=["PE_"])
                    tv, tvk = tv_r.next(); yo, yok = yo_r.next()
                    S.op("dve", lambda e: e.tensor_tensor(out=tv[:], in0=P4, in1=hin[:, 1, :, :].rearrange("p g f -> p (g f)"), op=ALU.add),
                         reads=[(hk, 1)], writes=["PE_", tvk])
                    S.op("pool", lambda e: e.tensor_tensor(out=yo[:].rearrange("p g f -> p (g f)"), in0=tv[:], in1=hin[:, 0, :, :].rearrange("p g f -> p (g f)"), op=ALU.mult),
                         reads=[tvk, (hk, 0)], writes=[yok])
                    for si, n in enumerate(pair):
                        S.op("actq", lambda e, si=si, n=n: e.dma_start(
                            out=snd[n][128 + g0:128 + g0 + G, :].rearrange("g (r f) -> r g f", f=128),
                            in_=yo[si * H:(si + 1) * H, :, :]), reads=[yok], writes=[("snd", n, g0)])

                for g0 in range(0, 128, G):
                    group(g0)
                S.emit(ph)

        def fnet_fft(t):
            with ExitStack() as ph:
                sbp = lambda name, shape, dt=F32: sb(name, shape, dt, ph)
                PA = ps_(ph, "PA", [128, 1024]); PB = ps_(ph, "PB", [128, 1024])
                PCs = [ps_(ph, f"PC{i}", [128, 512]) for i in range(2)]
                L = Ls[t]
                M1 = L // 128
                G = 512 // M1
                ci = consts_in[t]
                fR1 = sbp("fR1", [2 * M1, 2 * M1], BF16)
                S.op("poolq", lambda e: e.dma_start(out=fR1[:], in_=ci["fR1"]), writes=["fR1"])
                TG = sbp("fTG", [128, G, 2 * M1]); TGs = sbp("fTGs", [128, G, 2 * M1])
                rep_load(TG, ci["fT"], G, 2 * M1); rep_load(TGs, ci["fT"], G, 2 * M1, swap=True)
                kTG, kTGs = rep_keys(TG, G, False), rep_keys(TGs, G, True)
                mk = lambda nm, shape, dt=F32, n=2: Ring([sbp(f"{nm}{i}", shape, dt) for i in range(n)], nm)
                zin_r = mk("zin", [2 * M1, G, 128], BF16, n=3)
                fa_r = mk("fa", [128, G, 2 * M1]); fb_r = mk("fbb", [128, G, 2 * M1])
                fY_r = mk("fY", [128, 2, G, M1], BF16); fo_r = mk("fo", [128, G, M1], BF16)
                Pfs = [(PA[:, 0:G * 2 * M1].rearrange("p (g w) -> p g w", g=G), ["PA0", "PA1"]),
                       (PB[:, 0:G * 2 * M1].rearrange("p (g w) -> p g w", g=G), ["PB0", "PB1"])]
                cnt = [0]

                def group(sname, g0):
                    cnt[0] += 1
                    Pf, pfk = Pfs[cnt[0] % 2]
                    PC, pck = PCs[cnt[0] % 2], ("PC", cnt[0] % 2)
                    zin, zk = zin_r.next()
                    S.op("sp", lambda e: e.dma_start(out=zin[:], in_=scr_z[sname][:, g0:g0 + G, :]), writes=[zk])
                    for g in range(G):
                        S.op("pe", lambda e, g=g: e.matmul(Pf[:, g, :], lhsT=zin[:, g, :], rhs=fR1[:, :], start=True, stop=True),
                             reads=[zk, "fR1"], writes=pfk)
                    fa, fak = fa_r.next(); fbb, fbk = fb_r.next(); fY, fyk = fY_r.next()
                    for (a_, b_) in bsplit(G, 2 * M1):
                        S.op("dve", lambda e, a_=a_, b_=b_: e.tensor_tensor(out=fa[:, a_:b_, :], in0=Pf[:, a_:b_, :], in1=TG[:, a_:b_, :], op=ALU.mult), reads=kTG, writes=pfk + [fak])
                        S.op("dve", lambda e, a_=a_, b_=b_: e.tensor_tensor(out=fbb[:, a_:b_, :], in0=Pf[:, a_:b_, :], in1=TGs[:, a_:b_, :], op=ALU.mult), reads=kTGs, writes=pfk + [fbk])
                    S.op("pool", lambda e: e.tensor_tensor(out=fY[:, 0, :, :], in0=fa[:, :, 0:M1], in1=fa[:, :, M1:2 * M1], op=ALU.subtract), reads=[fak], writes=[(fyk, 0)])
                    S.op("pool", lambda e: e.tensor_tensor(out=fY[:, 1, :, :], in0=fbb[:, :, 0:M1], in1=fbb[:, :, M1:2 * M1], op=ALU.add), reads=[fbk], writes=[(fyk, 1)])
                    S.op("pe", lambda e: e.matmul(PC[:, 0:G * M1], lhsT=c2b[:, 0, :], rhs=fY[:, 0, :, :].rearrange("p g k -> p (g k)"), start=True, stop=False),
                         reads=[(fyk, 0), (fyk, 1), "c2b"], writes=[pck])
                    S.op("pe", lambda e: e.matmul(PC[:, 0:G * M1], lhsT=c2b[:, 1, :], rhs=fY[:, 1, :, :].rearrange("p g k -> p (g k)"), start=False, stop=True),
                         reads=[(fyk, 0), (fyk, 1), "c2b"], writes=[pck])
                    fo, fok = fo_r.next()
                    S.op("act", lambda e: e.activation(out=fo[:].rearrange("p g k -> p (g k)"), in_=PC[:, 0:G * M1], func=AF.Copy), writes=[pck, fok])
                    S.op("actq", lambda e: e.dma_start(out=snd[sname][g0:g0 + G, :].rearrange("g (k2 k1) -> k2 g k1", k1=M1), in_=fo[:]),
                         reads=[fok], writes=[("snd", sname, "f", g0)])

                for (sname, L_, _, _, _) in cfg.seqs:
                    if sname[0] != t:
                        continue
                    for g0 in range(0, 128, G):
                        group(sname, g0)
                S.emit(ph)

        for t in ("p", "s"):
            hyena_fft(t)
            fnet_fft(t)
        if debug or True:
            pass
        with ExitStack() as ph:
            for (sname, L, _, _, _) in cfg.seqs:
                def cc(e, sname=sname):
                    return e.collective_compute("AllGather", ALU.bypass, replica_groups=[list(range(8))],
                                                ins=[snd[sname]], outs=[gat[sname]])
                S.op("cc", cc, writes=[("gat", sname), "cc_chain"])
                if debug:
                    S.op("sp", lambda e, sname=sname: e.dma_start(out=dbg[sname], in_=snd[sname]), writes=[("dbg", sname)])
            S.emit(ph)

        if p1_only:
            return nc
        with ExitStack() as ph:
            sbp = lambda name, shape, dt=F32: sb(name, shape, dt, ph)
            PA = ps_(ph, "PA", [128, 1024]); PB = ps_(ph, "PB", [128, 1024])
            PC = ps_(ph, "PC", [128, 512]); PD = ps_(ph, "PD", [128, 512])
            PTs = [ps_(ph, f"PT{i}", [128, 1024], BF16) for i in range(2)]
            gft = sbp("gft", [128, D])
            S.op("sp", lambda e: e.dma_start(out=gft[:], in_=gfb), writes=["gft"])
            g1bb = sbp("g1bb", [128, D], BF16); g2bb = sbp("g2bb", [128, D], BF16)
            S.op("poolq", lambda e: e.dma_start(out=g1bb[:], in_=g1b), writes=["g1bb"])
            S.op("poolq", lambda e: e.dma_start(out=g2bb[:], in_=g2b), writes=["g2bb"])
            xt = sbp("xt2", [128, NS, D])
            hb_ring = Ring([sbp(f"hb2{i}", [128, D], BF16) for i in range(4)], "hb")
            st_ring = Ring([sbp(f"st2{i}", [128, 3 * NS]) for i in range(2)], "st")
            stf_ring = Ring([sbp(f"stf{i}", [128, 4]) for i in range(4)], "stf")
            hT = sbp("hT2", [128, KD, TT], BF16)
            NBIG = max(DFF // 128, 2 * KD + 16)
            MIX0 = 2 * KD
            big = sbp("big2", [128, NBIG, TT], BF16)
            mT = sbp("mT2", [128, KD, TT], BF16)
            tmpm = sbp("tmpm", [128, TT])
            rl_ring = Ring([sbp(f"rl{i}", [128, TT]) for i in range(2)], "rl")
            w_ring = Ring([sbp(f"wr{i}", [128, 4096], BF16) for i in range(4)], "wr")
            accs = [(PB[:, 0:512], "PB0"), (PB[:, 512:1024], "PB1"), (PC[:, :], "PC"), (PD[:, :], "PD")]
            fm_i = [0]

            def fmnext():
                fm_i[0] += 1
                b = fm_i[0] % 2
                return PA[:, b * 512:b * 512 + TT], ("PA", b)

            def wload(key):
                wt, wk = w_ring.next()
                src = wblock(key)
                S.op("sp", lambda e, wt=wt, src=src: e.dma_start(out=wt[:], in_=src), writes=[wk])
                return wt, wk

            def fm_weights(name, nblk, kc, ncols):
                per = ncols // 128
                for b in range(nblk):
                    wt, wk = wload((name, b))
                    wv = wt[:].rearrange("p (k n) -> p k n", k=kc)
                    for mm in range(per):
                        yield b * per + mm, (lambda k, wv=wv, mm=mm: wv[:, k, mm * 128:(mm + 1) * 128]), wk

            icol = 0
            for (sname, L, _, loff, nloc) in cfg.seqs:
                for u in range(nloc // TT):
                    lo = loff + u * TT
                    S.op("sp", lambda e, lo=lo: e.dma_start(out=xt[:], in_=x_loc[lo:lo + TT, :].rearrange("(s p) d -> p s d", p=128)),
                         writes=["xt"])
                    gview = gat[sname].rearrange("rc (b f) -> (rc b) f", f=TT)
                    for r in range(8):
                        for hf in range(2):
                            col = icol
                            icol += 1
                            S.op("poolq", lambda e, r=r, hf=hf, col=col, gview=gview: e.indirect_dma_start(
                                out=big[:, MIX0 + hf * 8 + r, :], out_offset=None, in_=gview,
                                in_offset=bass.IndirectOffsetOnAxis(ap=idxt[:, col:col + 1], axis=0)),
                                reads=["idxt"], writes=[("big", MIX0 + hf * 8 + r)])
                    norm_wide(PTs, xt, "xt", g1bb, "g1bb", hT, "hT", hb_ring, st_ring)
                    hreads = [("hT", k) for k in range(KD)]
                    for m, lw, wk in fm_weights("gate", 2 * D // 256, KD, 256):
                        ps, pk_ = fmnext()
                        for k in range(KD):
                            S.op("pe", lambda e, k=k, ps=ps, lw=lw: e.matmul(ps, lhsT=lw(k), rhs=hT[:, k, :], start=(k == 0), stop=(k == KD - 1)),
                                 reads=hreads + [wk], writes=[pk_])
                        S.op("act", lambda e, m=m, ps=ps: e.activation(out=big[:, m, :], in_=ps, func=AF.Sigmoid, bias=bgt[:, m:m + 1]),
                             reads=["bgt"], writes=[pk_, ("big", m)])
                    for br, (wname, hf) in enumerate((("map", 0), ("hout", 1))):
                        for m, lw, wk in fm_weights(wname, D // 512, KH, 512):
                            ps, pk_ = fmnext()
                            for k in range(KH):
                                S.op("pe", lambda e, k=k, ps=ps, lw=lw, hf=hf: e.matmul(ps, lhsT=lw(k), rhs=big[:, MIX0 + hf * 8 + k, :],
                                                                                       start=(k == 0), stop=(k == KH - 1)),
                                     reads=[("big", MIX0 + hf * 8 + r) for r in range(8)] + [wk], writes=[pk_])
                            if br == 0:
                                S.op("dve", lambda e, m=m, ps=ps: e.tensor_tensor(out=mT[:, m, :], in0=ps, in1=big[:, m, :], op=ALU.mult),
                                     reads=[("big", m)], writes=[pk_, ("mT", m)])
                            else:
                                S.op("dve", lambda e, m=m, ps=ps: e.tensor_tensor(out=tmpm[:], in0=ps, in1=big[:, KD + m, :], op=ALU.mult),
                                     reads=[("big", KD + m)], writes=[pk_, "tmpm"])
                                S.op("dve", lambda e, m=m: e.tensor_tensor(out=mT[:, m, :], in0=mT[:, m, :], in1=tmpm[:], op=ALU.add),
                                     reads=["tmpm", ("mT", m)], writes=[("mT", m)])
                    mreads = [("mT", k) for k in range(KD)]
                    nkb = KD // KB
                    for nb in range(D // 512):
                        for kb in range(nkb):
                            wt, wk = wload(("out", nb, kb))
                            wv = wt[:].rearrange("p (k n) -> p k n", k=KB)
                            for s_ in range(NS):
                                ps, pk_ = accs[s_ % 4]
                                for k in range(KB):
                                    kk = kb * KB + k
                                    S.op("pe", lambda e, k=k, kk=kk, s_=s_, ps=ps, wv=wv, kb=kb: e.matmul(
                                        ps, lhsT=mT[:, kk, s_ * 128:(s_ + 1) * 128], rhs=wv[:, k, :],
                                        start=(kb == 0 and k == 0), stop=(kb == nkb - 1 and k == KB - 1)), reads=mreads + [wk], writes=[pk_])
                        for s_ in range(NS):
                            ps, pk_ = accs[s_ % 4]
                            S.op("dve", lambda e, s_=s_, nb=nb, ps=ps: e.tensor_tensor(out=xt[:, s_, nb * 512:(nb + 1) * 512], in0=ps,
                                                                                       in1=xt[:, s_, nb * 512:(nb + 1) * 512], op=ALU.add),
                                 reads=["xt"], writes=[pk_, ("xt1", s_, nb)])
                    x1reads = [("xt1", s_, b) for s_ in range(NS) for b in range(D // 512)]
                    norm_wide(PTs, xt, "xt", g2bb, "g2bb", hT, "hT", hb_ring, st_ring, extra_reads=x1reads)
                    for f, lw, wk in fm_weights("m1", DFF // 256, KD, 256):
                        ps, pk_ = fmnext()
                        for k in range(KD):
                            S.op("pe", lambda e, k=k, ps=ps, lw=lw: e.matmul(ps, lhsT=lw(k), rhs=hT[:, k, :], start=(k == 0), stop=(k == KD - 1)),
                                 reads=hreads + [wk], writes=[pk_])
                        rl, rlk = rl_ring.next()
                        S.op("act", lambda e, ps=ps, rl=rl: e.activation(out=rl[:], in_=ps, func=AF.Relu), writes=[pk_, rlk])
                        S.op("pool", lambda e, f=f, rl=rl: e.tensor_tensor(out=big[:, f, :], in0=rl[:], in1=rl[:], op=ALU.mult),
                             reads=[rlk], writes=[("big", f)])
                    areads = [("big", f) for f in range(DFF // 128)]
                    nkb2 = (DFF // 128) // KB
                    for nb in range(D // 512):
                        for kb in range(nkb2):
                            wt, wk = wload(("m2", nb, kb))
                            wv = wt[:].rearrange("p (k n) -> p k n", k=KB)
                            for s_ in range(NS):
                                ps, pk_ = accs[s_ % 4]
                                for k in range(KB):
                                    f = kb * KB + k
                                    S.op("pe", lambda e, k=k, f=f, s_=s_, ps=ps, wv=wv, kb=kb: e.matmul(
                                        ps, lhsT=big[:, f, s_ * 128:(s_ + 1) * 128], rhs=wv[:, k, :],
                                        start=(kb == 0 and k == 0), stop=(kb == nkb2 - 1 and k == KB - 1)), reads=areads + [wk], writes=[pk_])
                        for s_ in range(NS):
                            ps, pk_ = accs[s_ % 4]
                            S.op("dve", lambda e, s_=s_, nb=nb, ps=ps: e.tensor_tensor(out=xt[:, s_, nb * 512:(nb + 1) * 512], in0=ps,
                                                                                       in1=xt[:, s_, nb * 512:(nb + 1) * 512], op=ALU.add),
                                 reads=[("xt1", s_, nb)], writes=[pk_, ("xt2", s_, nb)])
                    for s in range(NS):
                        stt, stk = stf_ring.next()
                        jb, jbk = hb_ring.next()
                        S.op("act", lambda e, s=s, stt=stt, jb=jb: e.activation(out=jb[:], in_=xt[:, s, :], func=AF.Square, accum_out=stt[:, 0:1]),
                             reads=[("xt2", s, b) for b in range(D // 512)], writes=[jbk, stk])
                        S.op("act", lambda e, stt=stt: e.activation(out=stt[:, 1:2], in_=stt[:, 0:1], func=AF.Sqrt, scale=1.0 / D, bias=epsc[:, 0:1]),
                             reads=[stk, "epsc"], writes=[(stk, "b")])
                        S.op("dve", lambda e, stt=stt: e.reciprocal(out=stt[:, 2:3], in_=stt[:, 1:2]), reads=[(stk, "b")], writes=[(stk, "c")])
                        S.op("dve", lambda e, s=s, stt=stt: e.scalar_tensor_tensor(out=xt[:, s, :], in0=xt[:, s, :], scalar=stt[:, 2:3], in1=gft[:],
                                                                                   op0=ALU.mult, op1=ALU.mult),
                             reads=[(stk, "c"), "gft"] + [("xt2", s, b) for b in range(D // 512)], writes=[("xt3", s)])
                    S.op("sp", lambda e, lo=lo: e.dma_start(out=y_loc[lo:lo + TT, :].rearrange("(s p) d -> p s d", p=128), in_=xt[:]),
                         reads=[("xt3", s) for s in range(NS)] + ["xt"], writes=["y_loc", "xt"] + [("xt1", s, b) for s in range(NS) for b in range(D // 512)]
                         + [("xt2", s, b) for s in range(NS) for b in range(D // 512)] + [("xt3", s) for s in range(NS)])
            S.op("sp", None, reads=["y_loc"])
            S.emit(ph)
    return nc


def make_inputs(cfg, inp):
    D, KD, TT = cfg.D, cfg.KD, cfg.TT
    f32 = lambda a: np.ascontiguousarray(np.asarray(a, dtype=np.float32))
    xp = f32(inp["x_prompt"]); xs = f32(inp["x_sample"])
    x_all = np.concatenate([xp.reshape(-1, D), xs.reshape(-1, D)], 0)
    col = lambda v: np.ascontiguousarray(f32(v).reshape(-1, 128).T)
    w_in = f32(inp["w_in"])[0]
    conv_w = f32(inp["conv_w"])[0]; conv_b = f32(inp["conv_b"])[0]
    common = dict(
        x_all=x_all, g1c=col(inp["norm1_g"][0]), g2c=col(inp["norm2_g"][0]),
        gfb=np.ascontiguousarray(np.broadcast_to(f32(inp["norm_f_g"])[None, :], (128, D))),
        bgc=col(inp["b_gate"][0]),
        g1b=np.ascontiguousarray(np.broadcast_to(f32(inp["norm1_g"][0])[None, :], (128, D))),
        fw1=f32(inp["filt_w1"])[0], fw2=f32(inp["filt_w2"])[0], fw3=f32(inp["filt_w3"])[0],
        fb=np.ascontiguousarray(np.stack([f32(inp["filt_b1"])[0], f32(inp["filt_b2"])[0], f32(inp["filt_b3"])[0],
                                          f32(inp["filt_freq"])[0]], 1)),
        ident=np.eye(128, dtype=np.float32), c2m=_common_consts(),
        g2b=np.ascontiguousarray(np.broadcast_to(f32(inp["norm2_g"][0])[None, :], (128, D))),
    )
    W = dict(gate=f32(inp["w_gate"])[0], map=f32(inp["w_fnet_map"])[0], hout=f32(inp["w_hyena_out"])[0],
             out=f32(inp["w_out"])[0], m1=f32(inp["w_mlp1"])[0], m2=f32(inp["w_mlp2"])[0])
    blocks = weight_blocks(cfg)
    common["w_all"] = np.concatenate([_extract_block(b_, W) for b_ in blocks], 0)
    for t, L in (("p", cfg.LP), ("s", cfg.LS)):
        c = _dft_consts(L)
        for k, v in c.items():
            common[f"{k}_{t}"] = np.ascontiguousarray(v)
        common[f"zT_{t}"] = _filter_embed(L)
    w4 = f32(inp["filt_w4"])[0]
    skip = f32(inp["hyena_skip"])[0]
    maps = []
    for c in range(8):
        m = dict(common)
        g, half = c // 2, c % 2
        cols = np.concatenate([np.arange(256 * g, 256 * g + 256), 1024 + 128 * c + np.arange(128),
                               2048 + 128 * c + np.arange(128), 3072 + 128 * c + np.arange(128)])
        m["w_in_c"] = np.ascontiguousarray(w_in[:, cols])
        cv = np.zeros((128, 12), np.float32)
        for q in range(3):
            ch = 1024 * q + 128 * c + np.arange(128)
            for j in range(3):
                cv[:, 4 * q + j] = conv_w[j, ch]
            cv[:, 4 * q + 3] = conv_b[ch]
        m["convc"] = cv
        m["fw4c"] = np.ascontiguousarray(np.concatenate([w4[:, 128 * c:128 * c + 128], w4[:, 1024 + 128 * c:1024 + 128 * c + 128]], 1))
        m["skc"] = np.ascontiguousarray(skip[128 * c:128 * c + 128][:, None])
        for t, L in (("p", cfg.LP), ("s", cfg.LS)):
            m[f"decT_{t}"] = _decay(L, c)
        cc = np.arange(256)[:, None].astype(np.float64)
        co = (128 * half + np.arange(128))[None, :].astype(np.float64)
        ang = 2 * np.pi * cc * co / 256.0
        cd = np.stack([np.cos(ang), -np.sin(ang)], 1)
        m["cdft"] = np.ascontiguousarray(cd.reshape(2, 128, 2, 128).transpose(1, 0, 2, 3)).astype(np.float32)
        parts = []
        for (n, L, xoff, loff, nloc) in cfg.seqs:
            parts.append(x_all[xoff + c * nloc: xoff + (c + 1) * nloc])
        m["x_loc"] = np.ascontiguousarray(np.concatenate(parts, 0))
        idx = np.zeros((128, 128), np.uint32)
        colI = 0
        p = np.arange(128)
        for (n, L, xoff, loff, nloc) in cfg.seqs:
            for u in range(nloc // TT):
                blk = c * (nloc // TT) + u
                for r in range(8):
                    for hf in range(2):
                        idx[:, colI] = (r * 256 + hf * 128 + p) * (L // TT) + blk
                        colI += 1
        assert colI <= 128
        m["idx"] = idx
        maps.append(m)
    return maps


def assemble(cfg, results):
    D = cfg.D
    yp = np.zeros((2, cfg.LP, D), np.float32)
    ys = np.zeros((2, cfg.LS, D), np.float32)
    for c in range(8):
        y = results[c]["y_loc"]
        for (n, L, xoff, loff, nloc) in cfg.seqs:
            dst = yp if n[0] == "p" else ys
            dst[int(n[1]), c * nloc:(c + 1) * nloc] = y[loff:loff + nloc]
    return yp, ys


def kernel(**inputs):
    cfg = Cfg()
    nc = build(cfg)
    maps = make_inputs(cfg, inputs)
    res = run_bass_kernel_spmd(nc, maps, core_ids=list(range(8)))
    return assemble(cfg, res.results)
```
